# Optimizing a Trainium2 kernel written in Bass

```python
import jax, jax.numpy as jnp
from jax import lax
import numpy as np

D_MODEL = 1024
BATCH = 32
SEQ = 2048
DEPTH = 2

CTX_LEN = 256
GRID_W = 64
RET_HEADS = 8
RET_DK = 128
RET_DV = 256
RET_QK = RET_HEADS * RET_DK
RET_V = RET_HEADS * RET_DV
CONV_WIDTH = D_MODEL
CONV_K = 3
N_BRANCH = 2
CHUNK = 128
D_FF = 256 * (-(-(8 * D_MODEL) // (3 * 256)))
ROPE_THETA = 10000.0
EPS = 1e-6
IN_COLS = 2 * RET_QK + 2 * RET_V + 3 * CONV_WIDTH + N_BRANCH * D_MODEL
IN_SPLITS = (RET_QK, 2 * RET_QK, 2 * RET_QK + RET_V, 2 * RET_QK + 2 * RET_V,
             2 * RET_QK + 2 * RET_V + CONV_WIDTH, 2 * RET_QK + 2 * RET_V + 2 * CONV_WIDTH,
             2 * RET_QK + 2 * RET_V + 3 * CONV_WIDTH)

kernel_name = 'hybrid_retention_shortconv_dit_block'


def rms_norm(x, w):
    xf = x.astype(jnp.float32)
    y = xf * lax.rsqrt(jnp.mean(xf * xf, axis=-1, keepdims=True) + EPS)
    return (y * w.astype(jnp.float32)).astype(x.dtype)


def modulation(cvec, w, b):
    m = jax.nn.silu(cvec) @ w + b
    return m.reshape(m.shape[:-1] + (6, D_MODEL))


def modulate(h, shift, scale):
    return h * (1.0 + scale) + shift


def _rotate(x, pos):
    d = x.shape[-1]
    freqs = ROPE_THETA ** (-jnp.arange(0, d, 2, dtype=jnp.float32) / d)
    ang = pos.astype(jnp.float32)[:, None] * freqs[None, :]
    cos = jnp.cos(ang)[:, None, :]
    sin = jnp.sin(ang)[:, None, :]
    xf = x.astype(jnp.float32)
    x1, x2 = xf[..., : d // 2], xf[..., d // 2:]
    return jnp.concatenate([x1 * cos - x2 * sin, x1 * sin + x2 * cos], axis=-1).astype(x.dtype)


def rope_2d(x, rows, cols):
    h = x.shape[-1] // 2
    return jnp.concatenate([_rotate(x[..., :h], rows), _rotate(x[..., h:], cols)], axis=-1)


def retention_chunkwise(q, k, v, log_g, s0):
    bsz, t, h, dk = q.shape
    dv = v.shape[-1]
    n = t // CHUNK
    idx = jnp.arange(CHUNK, dtype=jnp.float32)
    diff = idx[:, None] - idx[None, :]
    intra = jnp.exp(jnp.where(diff[None] >= 0, diff[None] * log_g[:, None, None], -jnp.inf))
    q_dec = jnp.exp((idx[:, None] + 1.0) * log_g[None, :])
    k_dec = jnp.exp((CHUNK - 1.0 - idx)[:, None] * log_g[None, :])
    chunk_dec = jnp.exp(CHUNK * log_g)

    def to_chunks(a):
        a = a.astype(jnp.float32).reshape(bsz, n, CHUNK, h, a.shape[-1])
        return jnp.moveaxis(a, 1, 0)

    def step(s, xs):
        qc, kc, vc = xs
        scores = jnp.einsum('bihd,bjhd->bhij', qc, kc) * intra
        o = jnp.einsum('bhij,bjhv->bihv', scores, vc)
        o = o + jnp.einsum('bihd,bhdv->bihv', qc, s) * q_dec[None, :, :, None]
        s = s * chunk_dec[None, :, None, None] + jnp.einsum(
            'bjhd,bjhv->bhdv', kc * k_dec[None, :, :, None], vc)
        return s, o

    s_final, o = lax.scan(step, s0.astype(jnp.float32), (to_chunks(q), to_chunks(k), to_chunks(v)))
    o = jnp.moveaxis(o, 0, 1).reshape(bsz, t, h, dv)
    return o, s_final


def bidir_retention(q, k, v, log_g, s0_fwd, s0_bwd):
    o_f, s_f = retention_chunkwise(q, k, v, log_g[0], s0_fwd)
    o_b, s_b = retention_chunkwise(q[:, ::-1], k[:, ::-1], v[:, ::-1], log_g[1], s0_bwd)
    return o_f + o_b[:, ::-1], s_f, s_b


def retention_final_states(k, v, log_g):
    t = k.shape[1]
    pos = jnp.arange(t, dtype=jnp.float32)
    w_f = jnp.exp((t - 1.0 - pos)[:, None] * log_g[0][None, :])
    w_b = jnp.exp(pos[:, None] * log_g[1][None, :])
    kf = k.astype(jnp.float32)
    vf = v.astype(jnp.float32)
    s_f = jnp.einsum('bthd,bthv->bhdv', kf * w_f[None, :, :, None], vf)
    s_b = jnp.einsum('bthd,bthv->bhdv', kf * w_b[None, :, :, None], vf)
    return s_f, s_b


def head_group_norm(o):
    mu = jnp.mean(o, axis=-1, keepdims=True)
    var = jnp.mean(jnp.square(o - mu), axis=-1, keepdims=True)
    return (o - mu) * lax.rsqrt(var + EPS)


def short_conv(u, w):
    t = u.shape[1]
    pad = CONV_K // 2
    up = jnp.pad(u, ((0, 0), (pad, pad), (0, 0)))
    out = up[:, 0:t] * w[0]
    for j in range(1, CONV_K):
        out = out + up[:, j:j + t] * w[j]
    return out


def merge_branches(o_ret, g, cb, cc, cu, gates, conv_w, w_ret_out, w_conv_out, w_o):
    bsz, t = g.shape[:2]
    y_ret = (head_group_norm(o_ret).reshape(bsz, t, RET_V).astype(g.dtype) * jax.nn.silu(g)) @ w_ret_out
    y_conv = (cb * short_conv(cc * cu, conv_w)) @ w_conv_out
    gr, gc = jnp.split(jax.nn.sigmoid(gates), 2, axis=-1)
    return (gr * y_ret + gc * y_conv) @ w_o


def hybrid_mixer(h_lat, h_ctx, w_in, decay_logit, conv_w, w_ret_out, w_conv_out, w_o, rows, cols, ctx_out):
    log_g = jax.nn.log_sigmoid(decay_logit.astype(jnp.float32))
    bsz, t_ctx = h_ctx.shape[:2]
    k_scale = RET_DK ** -0.5
    if ctx_out:
        q_c, k_c, v_c, g_c, b_c, c_c, u_c, gate_c = jnp.split(h_ctx @ w_in, IN_SPLITS, axis=-1)
        q_c = q_c.reshape(bsz, t_ctx, RET_HEADS, RET_DK)
        k_c = k_c.reshape(bsz, t_ctx, RET_HEADS, RET_DK) * k_scale
        v_c = v_c.reshape(bsz, t_ctx, RET_HEADS, RET_DV)
        s0 = jnp.zeros((bsz, RET_HEADS, RET_DK, RET_DV), jnp.float32)
        o_c, s_fwd, s_bwd = bidir_retention(q_c, k_c, v_c, log_g, s0, s0)
        y_ctx = merge_branches(o_c, g_c, b_c, c_c, u_c, gate_c, conv_w, w_ret_out, w_conv_out, w_o)
    else:
        k_c, v_c = jnp.split(h_ctx @ w_in[:, RET_QK:2 * RET_QK + RET_V], [RET_QK], axis=-1)
        k_c = k_c.reshape(bsz, t_ctx, RET_HEADS, RET_DK) * k_scale
        v_c = v_c.reshape(bsz, t_ctx, RET_HEADS, RET_DV)
        s_fwd, s_bwd = retention_final_states(k_c, v_c, log_g)
        y_ctx = None
    t = h_lat.shape[1]
    q, k, v, g, cb, cc, cu, gates = jnp.split(h_lat @ w_in, IN_SPLITS, axis=-1)
    q = rope_2d(q.reshape(bsz, t, RET_HEADS, RET_DK), rows, cols)
    k = rope_2d(k.reshape(bsz, t, RET_HEADS, RET_DK) * k_scale, rows, cols)
    v = v.reshape(bsz, t, RET_HEADS, RET_DV)
    o_lat, _, _ = bidir_retention(q, k, v, log_g, s_fwd, s_bwd)
    y_lat = merge_branches(o_lat, g, cb, cc, cu, gates, conv_w, w_ret_out, w_conv_out, w_o)
    return y_lat, y_ctx


def swiglu(h, w_in, w_out):
    gt, up = jnp.split(h @ w_in, 2, axis=-1)
    return (jax.nn.silu(gt) * up) @ w_out


def setup_inputs(seed: int = 0) -> dict:
    key = jax.random.key(seed)
    ks = jax.random.split(key, 17)
    f32 = jnp.float32

    def nrm(k, shape, scale):
        return jax.random.normal(k, shape, f32) * scale

    decay_init = jnp.log(2.0 ** (5.0 + jnp.arange(RET_HEADS, dtype=f32)) - 1.0)
    return {
        'x': nrm(ks[0], (BATCH, SEQ, D_MODEL), 1.0),
        'c': nrm(ks[1], (BATCH, D_MODEL), 1.0),
        'ctx': nrm(ks[2], (BATCH, CTX_LEN, D_MODEL), 1.0),
        'c_ctx': nrm(ks[3], (D_MODEL,), 1.0),
        'norm_mix_w': 1.0 + nrm(ks[4], (DEPTH, D_MODEL), 0.02),
        'ada_w': nrm(ks[5], (DEPTH, D_MODEL, 6 * D_MODEL), 0.5 * D_MODEL ** -0.5),
        'ada_b': nrm(ks[6], (DEPTH, 6 * D_MODEL), 0.01),
        'w_in': nrm(ks[7], (DEPTH, D_MODEL, IN_COLS), D_MODEL ** -0.5),
        'ret_decay_logit': decay_init + nrm(ks[8], (DEPTH, 2, RET_HEADS), 0.1),
        'conv_w': nrm(ks[9], (DEPTH, CONV_K, CONV_WIDTH), CONV_K ** -0.5),
        'w_ret_out': nrm(ks[10], (DEPTH, RET_V, D_MODEL), RET_V ** -0.5),
        'w_conv_out': nrm(ks[11], (DEPTH, CONV_WIDTH, D_MODEL), CONV_WIDTH ** -0.5),
        'w_o': nrm(ks[12], (DEPTH, D_MODEL, D_MODEL), D_MODEL ** -0.5),
        'norm_ffn_w': 1.0 + nrm(ks[13], (DEPTH, D_MODEL), 0.02),
        'w_ffn_in': nrm(ks[14], (DEPTH, D_MODEL, 2 * D_FF), D_MODEL ** -0.5),
        'w_ffn_out': nrm(ks[15], (DEPTH, D_FF, D_MODEL), D_FF ** -0.5),
        'final_norm_w': 1.0 + nrm(ks[16], (D_MODEL,), 0.02),
    }


def reference(x, c, ctx, c_ctx, norm_mix_w, ada_w, ada_b, w_in, ret_decay_logit, conv_w,
              w_ret_out, w_conv_out, w_o, norm_ffn_w, w_ffn_in, w_ffn_out, final_norm_w):
    seq = x.shape[1]
    n_rows = seq // GRID_W
    rows = jnp.broadcast_to(jnp.arange(n_rows)[:, None], (n_rows, GRID_W)).reshape(-1)
    cols = jnp.broadcast_to(jnp.arange(GRID_W)[None, :], (n_rows, GRID_W)).reshape(-1)
    xc = ctx
    for l in range(DEPTH):
        last = l == DEPTH - 1
        m_lat = modulation(c, ada_w[l], ada_b[l])[:, :, None, :]
        m_ctx = modulation(c_ctx, ada_w[l], ada_b[l])
        h_lat = modulate(rms_norm(x, norm_mix_w[l]), m_lat[:, 0], m_lat[:, 1])
        h_ctx = modulate(rms_norm(xc, norm_mix_w[l]), m_ctx[0], m_ctx[1])
        y_lat, y_ctx = hybrid_mixer(h_lat, h_ctx, w_in[l], ret_decay_logit[l], conv_w[l],
                                    w_ret_out[l], w_conv_out[l], w_o[l], rows, cols, not last)
        x = x + m_lat[:, 2] * y_lat
        h = modulate(rms_norm(x, norm_ffn_w[l]), m_lat[:, 3], m_lat[:, 4])
        x = x + m_lat[:, 5] * swiglu(h, w_ffn_in[l], w_ffn_out[l])
        if not last:
            xc = xc + m_ctx[2] * y_ctx
            hc = modulate(rms_norm(xc, norm_ffn_w[l]), m_ctx[3], m_ctx[4])
            xc = xc + m_ctx[5] * swiglu(hc, w_ffn_in[l], w_ffn_out[l])
    return rms_norm(x, final_norm_w)
```

```python
import contextlib
import math
import numpy as np
import concourse.bass as bass
import concourse.mybir as mybir
from concourse.bass_utils import run_bass_kernel_spmd

F32 = mybir.dt.float32
BF16 = mybir.dt.bfloat16
I32 = mybir.dt.int32
AF = mybir.ActivationFunctionType
ALU = mybir.AluOpType

D = 1024
T = 2048
TCX = 256
H = 8
DK = 128
DV = 256
DFF = 2816
NKF = 22
G = 2
TG = 128 * G
NCH = T // 128
NCC = TCX // 128
EPS = 1e-6
LN_KS = -0.5 * math.log(128.0)
NW = 3
N_CORES = 8
QUEUES = ("pe", "act", "dve", "pool", "sp")


class Buf:
    __slots__ = ("name", "last_w", "readers", "arena")

    def __init__(self, name, arena=False):
        self.name = name
        self.last_w = None
        self.readers = []
        self.arena = arena


class Op:
    __slots__ = ("q", "fn", "deps", "raw", "dma", "idx", "inc_no", "dma_sem", "dma_val")

    def __init__(self, q, fn, dma):
        self.q = q
        self.fn = fn
        self.deps = set()
        self.raw = set()
        self.dma = dma
        self.inc_no = None
        self.dma_sem = None
        self.dma_val = None


class Prog:
    def __init__(self, nc, n_dma_sems=28):
        self.nc = nc
        self.ops = []
        self.n_dma_sems = n_dma_sems
        self.dry = False
        self.ARENA = Buf("ARENA")

    def add(self, q, fn, reads=(), writes=(), dma=False):
        if self.dry:
            return None
        reads = list(reads)
        if any(b.arena for b in reads) or any(b.arena for b in writes):
            reads.append(self.ARENA)
        op = Op(q, fn, dma)
        op.idx = len(self.ops)
        for r in reads:
            if r.last_w is not None:
                op.deps.add(r.last_w)
                op.raw.add(r.last_w)
        for w in writes:
            if w.last_w is not None:
                op.deps.add(w.last_w)
            latest = {}
            for rd in w.readers:
                if self.ops[rd].dma:
                    op.deps.add(rd)
                else:
                    latest[self.ops[rd].q] = rd
            op.deps.update(latest.values())
        for r in reads:
            r.readers.append(op.idx)
        for w in writes:
            w.last_w = op.idx
            w.readers = []
        op.deps.discard(op.idx)
        self.ops.append(op)
        return op

    def emit(self, final_ops=()):
        nc = self.nc
        ops = self.ops
        need_inc = [False] * len(ops)
        for op in ops:
            for d in op.deps:
                dop = ops[d]
                if not dop.dma and (dop.q != op.q or op.dma or op.q != "pe"):
                    need_inc[d] = True
        cnt = {q: 0 for q in QUEUES}
        for op in ops:
            if not op.dma and need_inc[op.idx]:
                cnt[op.q] += 1
                op.inc_no = cnt[op.q]
        with contextlib.ExitStack() as es:
            qsem = {q: es.enter_context(nc.semaphore("s_" + q)) for q in QUEUES}
            dsems = [es.enter_context(nc.semaphore("d%d" % i)) for i in range(self.n_dma_sems)]
            dval = [0] * self.n_dma_sems
            dlast = [None] * self.n_dma_sems
            n_pool = 8
            pools = {"pool": list(range(0, n_pool)), "sp": list(range(n_pool, self.n_dma_sems))}
            kq = {"pool": 0, "sp": 0}
            for op in ops:
                if op.dma:
                    pl = pools[op.q]
                    s = pl[kq[op.q] % len(pl)]
                    kq[op.q] += 1
                    op.dma_sem = s
                    if dlast[s] is not None:
                        op.deps.add(dlast[s])
                    dval[s] += 16
                    op.dma_val = dval[s]
                    dlast[s] = op.idx
            byq = {q: [o for o in ops if o.q == q] for q in QUEUES}
            block = es.enter_context(nc.Block())

            def make(q):
                def body(eng):
                    waited = {}
                    for op in byq[q]:
                        for d in sorted(op.deps):
                            dop = ops[d]
                            if dop.dma:
                                key, val, sem = ("d", dop.dma_sem), dop.dma_val, dsems[dop.dma_sem]
                            else:
                                if dop.q == q and not op.dma and q == "pe":
                                    continue
                                key, val, sem = ("q", dop.q), dop.inc_no, qsem[dop.q]
                            if waited.get(key, 0) >= val:
                                continue
                            eng.wait_ge(sem, val)
                            waited[key] = val
                        ins = op.fn(eng)
                        if op.dma:
                            ins.then_inc(dsems[op.dma_sem], 16)
                        elif op.inc_no is not None:
                            ins.then_inc(qsem[q], 1)
                    if q == "sp":
                        for dop in final_ops:
                            eng.wait_ge(dsems[dop.dma_sem], dop.dma_val)
                return body

            block.tensor(make("pe"))
            block.scalar(make("act"))
            block.vector(make("dve"))
            block.gpsimd(make("pool"))
            block.sync(make("sp"))


class _Stop(Exception):
    pass


def build(NB, dbg=None, stop=99):
    nc = bass.Bass("TRN2", target_bir_lowering=False)
    NBC = NB + 1

    def din(name, shape, dt=F32):
        return nc.dram_tensor(name, list(shape), dt, kind="ExternalInput").ap()

    x_d = din("x", [NB, T, D])
    ctx_d = din("ctx", [NB, TCX, D])
    cT_d = din("cT", [128, 8, NBC])
    adaw_d = din("ada_w", [2, D, 6 * D])
    adab_d = din("ada_bT", [128, 2, 48])
    nmw_d = din("nmwT", [128, 2, 8])
    nfw_d = din("nfwT", [128, 2, 8])
    cw_d = din("convwT", [128, 2, 3, 8])
    fnw_d = din("fnw", [D])
    lg_d = din("decay", [32])
    rowcs_d = din("rowcs", [128, 2, 32])
    colcs_d = din("colcs", [128, 2, 64])
    ident_d = din("ident", [128, 128])
    win_d = din("w_in_ext", [2, D, 13312])
    wro_d = din("w_ret_out", [2, 2 * D, D])
    wco_d = din("w_conv_out", [2, D, D])
    wo_d = din("w_o", [2, D, D])
    wfi_d = din("w_ffn_in_ext", [2, D, 2 * DFF])
    wfo_d = din("w_ffn_out", [2, DFF, D])
    out_d = nc.dram_tensor("out", [NB, T, D], F32, kind="ExternalOutput").ap()
    sbd_d = nc.dram_tensor("sbd", [NCH + NCC, 128, H * DV], BF16, kind="Internal").ap()
    dbg_out = {}

    P = Prog(nc)

    total = (nc.sbuf_bytes_remaining // 64) * 64 - 256
    AR = nc.alloc_sbuf_tensor("arena", [128, total // 2], BF16)
    cur = [0]

    def carve(shape, dt, base=None):
        nel = int(np.prod(shape))
        nbytes = nel * (2 if dt == BF16 else 4)
        o = cur[0] if base is None else base[0]
        o = (o + 31) // 32 * 32
        v = AR[:, o // 2:(o + nbytes) // 2]
        if dt != BF16:
            v = v.bitcast(dt)
        if len(shape) == 2:
            v = v.rearrange("p (a b) -> p a b", a=shape[0])
        elif len(shape) == 3:
            v = v.rearrange("p (a b c) -> p a b c", a=shape[0], b=shape[1])
        if base is None:
            cur[0] = o + nbytes
        else:
            base[0] = o + nbytes
        assert o + nbytes <= total, ("SBUF overflow", o + nbytes, total)
        return v

    X = carve([NCH, D], F32)
    XC = carve([NCC, D], F32)
    FNW = carve([D], F32)
    G2 = carve([D], F32)
    G5 = carve([D], F32)
    WS = [carve([8, 512], BF16) for _ in range(NW)]
    hT = carve([8, TG + 2], BF16)
    onT = carve([16, TG], BF16)
    SBr = carve([H, DV], F32)
    SFr = carve([H, DV], F32)
    Sfb = carve([H * DV], BF16)
    Sbb = [carve([H * DV], BF16) for _ in range(2)]
    MT = carve([H, 128], F32)
    QDF = carve([H, 128], BF16)
    QDB = carve([H, 128], BF16)
    COSg = carve([TG], F32)
    SINg = carve([TG], F32)
    identf = carve([128], F32)
    onesb = carve([128], BF16)
    identb = carve([128], BF16)
    Dm = carve([128], F32)
    I1 = carve([128], F32)
    I2 = carve([128], F32)
    Jp = carve([2], F32)
    LGt = carve([32], F32)
    SPt = carve([32], F32)
    KDF = carve([H], F32)
    KDB = carve([H], F32)
    CDF = carve([H], F32)
    CDB = carve([H], F32)
    MTa = carve([2, 48, NBC], F32)
    WM = carve([2, 2, 8 * NBC], F32)
    NMW = carve([2, 8], F32)
    NFW = carve([2, 8], F32)
    ADB = carve([2, 48], F32)
    CW = carve([2, 3, 8], F32)
    ROWCS = carve([2, 32], F32)
    COLCS = carve([2, 64], F32)
    scT = carve([8, NBC], BF16)
    cTt = carve([8, NBC], F32)
    HALO = carve([8, NCH // G + NCC // G, 2], BF16)
    sm = carve([64], F32)
    smi = carve([16], I32)
    st6 = carve([H, 6], F32)
    mv = carve([H, 2], F32)
    tmpi = carve([128], I32)
    arena0 = cur[0]

    def arena_alloc():
        return [arena0]

    aR = arena_alloc()
    qT = carve([H, TG], BF16, aR)
    kT = carve([H, TG], BF16, aR)
    vv = carve([G, H * DV], BF16, aR)
    kdt = [carve([H, 128], BF16, aR) for _ in range(2)]
    qf = carve([H, 128], BF16, aR)
    qb = carve([H, 128], BF16, aR)
    PT = [carve([H, 128], BF16, aR) for _ in range(2)]
    onb = carve([H * DV], BF16, aR)
    ropeA = carve([512], F32, aR)
    off_ropeB = (aR[0] + 31) // 32 * 32
    ropeB = carve([512], F32, aR)
    sgt = [carve([512], BF16, [off_ropeB]), carve([512], BF16, [off_ropeB + 1024])]
    xs = [carve([D], BF16, aR) for _ in range(2)]
    aT = arena_alloc()
    tabA = carve([128], F32, aT)
    tabB = carve([128], F32, aT)
    dg = [carve([128], F32, aT) for _ in range(2)]
    dgh = [carve([128], BF16, aT) for _ in range(2)]
    dgl = [carve([128], BF16, aT) for _ in range(2)]

    aC = arena_alloc()
    macc = carve([G, D], F32, aC)
    tgr = carve([G, D], BF16, aC)
    tgc = carve([G, D], BF16, aC)
    uT = carve([8, TG + 2], BF16, aC)
    cv = carve([8, TG], BF16, aC)
    ycT = carve([8, TG], BF16, aC)
    cvt = [carve([TG], F32, aC)] * 2
    t2 = [carve([512], F32, aC) for _ in range(2)]
    mm = carve([G, D], BF16, aC)
    mT = carve([8, TG], BF16, aC)
    wtmpC = t2

    aF = arena_alloc()
    h2T = carve([8, TG], BF16, aF)
    hidT = carve([NKF, TG], BF16, aF)
    sgtF = [carve([512], BF16, aF) for _ in range(2)]
    wtmpF = [carve([512], F32, aF) for _ in range(2)]
    Yt = [carve([D], F32, aF) for _ in range(2)]
    xsF = [carve([D], BF16, aF) for _ in range(2)]

    PF = [nc.alloc_psum_tensor("pf%d" % i, [128, 512], F32) for i in range(6)]
    PB = [nc.alloc_psum_tensor("pb%d" % i, [128, 8, 128], BF16) for i in range(2)]
    bPF = [Buf("pf%d" % i) for i in range(6)]
    bPB = [Buf("pb%d" % i) for i in range(2)]
    rr = {"pf": 0, "pb": 0}

    def pf():
        i = rr["pf"] % 6
        rr["pf"] += 1
        return PF[i], bPF[i]

    def pb():
        i = rr["pb"] % 2
        rr["pb"] += 1
        return PB[i], bPB[i]

    def B(name, arena=False):
        return Buf(name, arena)

    bX = [B("X%d" % i) for i in range(NCH)]
    bXC = [B("XC%d" % i) for i in range(NCC)]
    bWS = [B("W%d" % i) for i in range(NW)]
    bhT, bonT, bSBr, bSFr, bSfb = B("hT"), B("onT"), B("SBr"), B("SFr"), B("Sfb")
    bSbb = [B("Sbb0"), B("Sbb1")]
    bTAB, bMOD, bG2, bG5, bROPE, bCONST, bHALO, bSM = (B("TAB"), B("MOD"), B("G2"), B("G5"), B("ROPE"),
                                                       B("CONST"), B("HALO"), B("SM"))
    bFNW, bjunk = B("FNW"), B("junk")
    bSBD = [B("SBD%d" % i) for i in range(NCH + NCC)]
    A = lambda n: B(n, True)
    bqT, bkT, bvv, bqf, bqb, bonb, bropeA, bropeB = A("qT"), A("kT"), A("vv"), A("qf"), A("qb"), A("onb"), A("rA"), A("rB")
    bkdt = [A("kdt0"), A("kdt1")]
    bPT = [A("PT0"), A("PT1")]
    bsgt = [bropeB, bropeB]
    bxs = [A("xs0"), A("xs1")]
    btab = A("tabAB")
    bdg = [A("dg0"), A("dg1")]
    bdgh = [A("dgh0"), A("dgh1")]
    bdgl = [A("dgl0"), A("dgl1")]
    bmacc, btgr, btgc, buT, bcv, bycT, bmm, bmT = A("macc"), A("tgr"), A("tgc"), A("uT"), A("cv"), A("ycT"), A("mm"), A("mT")
    bcvt = [A("cvt0")] * 2
    bt2 = [A("t20"), A("t21")]
    bwtC = bt2
    bh2T, bhidT = A("h2T"), A("hidT")
    bsgF = [A("sgF0"), A("sgF1")]
    bwtF = [A("wtF0"), A("wtF1")]
    bYt = [A("Yt0"), A("Yt1")]
    bxsF = [A("xsF0"), A("xsF1")]
    slot = {}

    def nxt(key, n=2):
        v = slot.get(key, 0)
        slot[key] = v + 1
        return v % n

    def pe(fn, r=(), w=()):
        return P.add("pe", fn, r, w)

    def act(fn, r=(), w=()):
        return P.add("act", fn, r, w)

    def dve(fn, r=(), w=()):
        return P.add("dve", fn, r, w)

    def dma(q, out, in_, r=(), w=()):
        return P.add(q, lambda e: e.dma_start(out=out, in_=in_), r, w, dma=True)

    def accfence(q):
        if q == "act":
            act(lambda e: e.activation(out=sm[:, 62:63], in_=sm[:, 61:62], func=AF.Copy), r=[bSM], w=[bSM])
        else:
            dve(lambda e: e.tensor_copy(out=sm[:, 60:61], in_=sm[:, 59:60]), r=[bSM], w=[bSM])

    def arena_barrier():
        dve(lambda e: e.memset(sm[:, 63:64], 0.0), r=(), w=[P.ARENA])

    def dump(name, view, bufs, shape, dt=F32):
        if dbg is None or name not in dbg or P.dry:
            return
        t = nc.dram_tensor("dbg_" + name, list(shape), dt, kind="ExternalOutput").ap()
        dbg_out[name] = t
        o = dma("sp", t, view, r=bufs)
        finals.append(o)

    finals = []

    wlist = []
    wstate = {"k": 0, "loaded": 0}

    def use_tile(src, nk):
        if P.dry:
            wlist.append((src, nk))
            return WS[0], bWS[0]
        k = wstate["k"]
        wstate["k"] += 1
        while wstate["loaded"] < min(k + NW, len(wlist)):
            j = wstate["loaded"]
            s = j % NW
            sj, nkj = wlist[j]
            dma("pool", WS[s][:, 0:nkj, :], sj, w=[bWS[s]])
            wstate["loaded"] += 1
        return WS[k % NW], bWS[k % NW]

    def wtile(w3, l, k0, nk, c0):
        v = w3[l].rearrange("(k p) n -> p k n", p=128)
        return v[:, k0:k0 + nk, c0:c0 + 512], nk

    def rsqrt(dst, src, n, rbufs, wbuf):
        yi = smi[:, 8:8 + n]
        yf = yi.bitcast(F32)
        a = sm[:, 48:48 + n]
        dve(lambda e: e.tensor_scalar(out=yi, in0=src.bitcast(I32), scalar1=-0.5, scalar2=1597463007.0, op0=ALU.mult, op1=ALU.add),
            r=rbufs, w=[bSM])
        for it in range(3):
            dve(lambda e: e.tensor_tensor(out=a, in0=yf, in1=yf, op=ALU.mult), r=[bSM], w=[bSM])
            dve(lambda e: e.tensor_tensor(out=a, in0=a, in1=src, op=ALU.mult), r=[bSM] + list(rbufs), w=[bSM])
            dve(lambda e: e.tensor_scalar(out=a, in0=a, scalar1=-0.5, scalar2=1.5, op0=ALU.mult, op1=ALU.add), r=[bSM], w=[bSM])
            if it < 2:
                dve(lambda e: e.tensor_tensor(out=yf, in0=yf, in1=a, op=ALU.mult), r=[bSM], w=[bSM])
            else:
                dve(lambda e: e.tensor_tensor(out=dst, in0=yf, in1=a, op=ALU.mult), r=[bSM], w=[wbuf])

    def norm_T(xt, bx, l, which, bidx, dstT, bdst, col0, xsl, bxsl):
        s = nxt("xs")
        ssv = sm[:, 0:1]
        vv_ = sm[:, 1:2]
        rs = sm[:, 2:3]
        act(lambda e: e.activation(out=xsl[s][:], in_=xt, func=AF.Square, accum_out=ssv), r=[bx], w=[bxsl[s], bSM])
        accfence("act")
        act(lambda e: e.activation(out=vv_, in_=ssv, func=AF.Identity, scale=1.0 / D, bias=EPS), r=[bSM], w=[bSM])
        rsqrt(rs, vv_, 1, [bSM], bSM)
        act(lambda e: e.activation(out=xsl[s][:], in_=xt, func=AF.Identity, scale=rs), r=[bx, bSM], w=[bxsl[s]])
        pt, bpt = pb()
        for k in range(8):
            pe(lambda e, k=k: e.transpose(out=pt[:, k, :], in_=xsl[s][:, k * 128:(k + 1) * 128], identity=identb[:]),
               r=[bxsl[s], bCONST], w=[bpt])
        wsel = 0 if which == 0 else 1
        for k in range(8):
            wm = WM[:, l, wsel, k * NBC + bidx:k * NBC + bidx + 1]
            sh = MTa[:, l, (0 if which == 0 else 24) + k, bidx:bidx + 1]
            dve(lambda e, k=k, wm=wm, sh=sh: e.tensor_scalar(out=dstT[:, k, col0:col0 + 128], in0=pt[:, k, :], scalar1=wm, scalar2=sh,
                                                             op0=ALU.mult, op1=ALU.add), r=[bpt, bMOD], w=[bdst])

    def setup():
        dma("sp", identf[:], ident_d, w=[bCONST])
        dma("sp", cTt[:], cT_d, w=[bMOD])
        dma("sp", ADB[:], adab_d, w=[bMOD])
        dma("sp", NMW[:], nmw_d, w=[bMOD])
        dma("sp", NFW[:], nfw_d, w=[bMOD])
        dma("sp", CW[:], cw_d, w=[bMOD])
        dma("sp", ROWCS[:], rowcs_d, w=[bROPE])
        dma("sp", COLCS[:], colcs_d, w=[bROPE])
        dma("sp", LGt[:], bass.AP(lg_d.tensor, 0, [[0, 128], [1, 32]]), w=[bTAB])
        dma("sp", FNW[:], bass.AP(fnw_d.tensor, 0, [[0, 128], [1, D]]), w=[bFNW])
        dve(lambda e: e.tensor_copy(out=identb[:], in_=identf[:]), r=[bCONST], w=[bCONST])
        dve(lambda e: e.memset(onesb[:], 1.0), w=[bCONST])
        dve(lambda e: e.memset(sm[:], 0.0), w=[bSM])
        dve(lambda e: e.memset(hT[:], 0.0), w=[bhT])
        P.add("pool", lambda e: e.iota(tmpi[:], pattern=[[1, 128]], base=0, channel_multiplier=-1), (), [bSM])
        dve(lambda e: e.tensor_copy(out=Dm[:], in_=tmpi[:]), r=[bSM], w=[bCONST])
        P.add("pool", lambda e: e.iota(tmpi[:], pattern=[[1, 128]], base=1, channel_multiplier=0), (), [bSM])
        dve(lambda e: e.tensor_copy(out=I1[:], in_=tmpi[:]), r=[bSM], w=[bCONST])
        dve(lambda e: e.tensor_scalar(out=I2[:], in0=I1[:], scalar1=-1.0, scalar2=129.0, op0=ALU.mult, op1=ALU.add), r=[bCONST], w=[bCONST])
        P.add("pool", lambda e: e.iota(tmpi[:, 0:1], pattern=[[1, 1]], base=0, channel_multiplier=1), (), [bSM])
        dve(lambda e: e.tensor_copy(out=Jp[:, 0:1], in_=tmpi[:, 0:1]), r=[bSM], w=[bCONST])
        dve(lambda e: e.tensor_scalar(out=Jp[:, 1:2], in0=Jp[:, 0:1], scalar1=-1.0, scalar2=127.0, op0=ALU.mult, op1=ALU.add), r=[bCONST], w=[bCONST])
        act(lambda e: e.activation(out=SPt[:], in_=LGt[:], func=AF.Exp, scale=-1.0), r=[bTAB], w=[bTAB])
        act(lambda e: e.activation(out=SPt[:], in_=SPt[:], func=AF.Ln, bias=1.0), r=[bTAB], w=[bTAB])
        dve(lambda e: e.tensor_scalar(out=LGt[:], in0=SPt[:], scalar1=-1.0, scalar2=None, op0=ALU.mult), r=[bTAB], w=[bTAB])
        for (dst, j) in ((COSg, 0), (SINg, 1)):
            src = COLCS[64:128, j, :].unsqueeze(1).broadcast_to([64, TG // 64, 64])
            dve(lambda e, dst=dst, src=src: e.tensor_copy(out=dst[64:128, :].rearrange("p (a b) -> p a b", b=64), in_=src),
                r=[bROPE], w=[bROPE])
        act(lambda e: e.activation(out=scT[:], in_=cTt[:], func=AF.Silu), r=[bMOD], w=[bMOD])
        for l in range(2):
            bank, bb = pf()
            for t in range(12):
                W, bW = use_tile(*wtile(adaw_d, l, 0, 8, t * 512))
                for cc in range(4):
                    j = t * 4 + cc
                    for k in range(8):
                        pe(lambda e, W=W, cc=cc, k=k, j=j, bank=bank: e.matmul(bank[:, j * NBC:(j + 1) * NBC], lhsT=W[:, k, cc * 128:(cc + 1) * 128],
                                                                    rhs=scT[:, k, :], start=(k == 0), stop=(k == 7)),
                           r=[bW, bMOD], w=[bb])
            dve(lambda e, l=l, bank=bank: e.tensor_tensor(out=MTa[:, l, :, :], in0=bank[:, 0:48 * NBC].rearrange("p (j b) -> p j b", b=NBC),
                                               in1=ADB[:, l, :].unsqueeze(2).broadcast_to([128, 48, NBC]), op=ALU.add),
                r=[bb, bMOD], w=[bMOD])
            for wsel, (nw_, j0) in enumerate(((NMW, 8), (NFW, 32))):
                dve(lambda e, l=l, wsel=wsel, nw_=nw_, j0=j0: e.scalar_tensor_tensor(
                    out=WM[:, l, wsel, :].rearrange("p (k b) -> p k b", b=NBC), in0=MTa[:, l, j0:j0 + 8, :], scalar=1.0,
                    in1=nw_[:, l, :].unsqueeze(2).broadcast_to([128, 8, NBC]), op0=ALU.add, op1=ALU.mult), r=[bMOD], w=[bMOD])

    def decay_tables(l):
        for h in range(H):
            fi = l * 16 + h
            bi = l * 16 + 8 + h
            act(lambda e, fi=fi: e.activation(out=tabA[:], in_=Dm[:], func=AF.Exp, scale=LGt[:, fi:fi + 1], bias=LN_KS),
                r=[bCONST, bTAB], w=[btab])
            dve(lambda e: e.scalar_tensor_tensor(out=tabA[:], in0=Dm[:], scalar=0.0, in1=tabA[:], op0=ALU.is_ge, op1=ALU.mult),
                r=[bCONST, btab], w=[btab])
            act(lambda e, bi=bi: e.activation(out=tabB[:], in_=Dm[:], func=AF.Exp, scale=SPt[:, bi:bi + 1], bias=LN_KS),
                r=[bCONST, bTAB], w=[btab])
            dve(lambda e: e.scalar_tensor_tensor(out=tabB[:], in0=Dm[:], scalar=0.0, in1=tabB[:], op0=ALU.is_le, op1=ALU.mult),
                r=[bCONST, btab], w=[btab])
            dve(lambda e, h=h: e.tensor_tensor(out=MT[:, h, :], in0=tabA[:], in1=tabB[:], op=ALU.add), r=[btab], w=[bTAB])
            act(lambda e, h=h, fi=fi: e.activation(out=QDF[:, h, :], in_=I1[:], func=AF.Exp, scale=LGt[:, fi:fi + 1]),
                r=[bCONST, bTAB], w=[bTAB])
            act(lambda e, h=h, bi=bi: e.activation(out=QDB[:, h, :], in_=I2[:], func=AF.Exp, scale=LGt[:, bi:bi + 1]),
                r=[bCONST, bTAB], w=[bTAB])
            act(lambda e, h=h, fi=fi: e.activation(out=KDF[:, h:h + 1], in_=Jp[:, 1:2], func=AF.Exp, scale=LGt[:, fi:fi + 1], bias=LN_KS),
                r=[bCONST, bTAB], w=[bTAB])
            act(lambda e, h=h, bi=bi: e.activation(out=KDB[:, h:h + 1], in_=Jp[:, 0:1], func=AF.Exp, scale=LGt[:, bi:bi + 1], bias=LN_KS),
                r=[bCONST, bTAB], w=[bTAB])
        act(lambda e: e.activation(out=CDF[:], in_=LGt[:, l * 16:l * 16 + 8], func=AF.Exp, scale=128.0), r=[bTAB], w=[bTAB])
        act(lambda e: e.activation(out=CDB[:], in_=LGt[:, l * 16 + 8:l * 16 + 16], func=AF.Exp, scale=128.0), r=[bTAB], w=[bTAB])

    def gate_rows(l, bidx):
        for (dst, bdst, j0, sc) in ((G2, bG2, 16, 0.5), (G5, bG5, 40, 1.0)):
            for half in range(2):
                bank, bb = pf()
                for kk in range(4):
                    k = half * 4 + kk
                    s = nxt("dg")
                    col = MTa[:, l, j0 + k, bidx:bidx + 1]
                    dve(lambda e, s=s, col=col, sc=sc: e.tensor_scalar(out=dg[s][:], in0=identf[:], scalar1=col, scalar2=sc,
                                                                       op0=ALU.mult, op1=ALU.mult), r=[bCONST, bMOD], w=[bdg[s]])
                    dve(lambda e, s=s: e.tensor_copy(out=dgh[s][:], in_=dg[s][:]), r=[bdg[s]], w=[bdgh[s]])
                    dve(lambda e, s=s: e.tensor_tensor(out=dgl[s][:], in0=dg[s][:], in1=dgh[s][:], op=ALU.subtract), r=[bdg[s], bdgh[s]], w=[bdgl[s]])
                    pe(lambda e, s=s, kk=kk, bank=bank: e.matmul(bank[:, kk * 128:(kk + 1) * 128], lhsT=onesb[:], rhs=dgh[s][:], start=True, stop=False),
                       r=[bCONST, bdgh[s]], w=[bb])
                    pe(lambda e, s=s, kk=kk, bank=bank: e.matmul(bank[:, kk * 128:(kk + 1) * 128], lhsT=onesb[:], rhs=dgl[s][:], start=False, stop=True),
                       r=[bCONST, bdgl[s]], w=[bb])
                act(lambda e, half=half, dst=dst, bank=bank: e.activation(out=dst[:, half * 512:(half + 1) * 512], in_=bank[:], func=AF.Copy),
                    r=[bb], w=[bdst])

    class Seq:
        pass

    def mkseq(kind, b):
        s = Seq()
        s.kind = kind
        if kind == "lat":
            s.n, s.Xt, s.bXt, s.rope, s.bidx, s.sbd0, s.halo0 = NCH, X, bX, True, b, 0, 0
        else:
            s.n, s.Xt, s.bXt, s.rope, s.bidx, s.sbd0, s.halo0 = NCC, XC, bXC, False, NB, NCH, NCH // G
        s.ng = s.n // G
        return s

    def rope_tables(g):
        for (dst, j) in ((COSg, 0), (SINg, 1)):
            src = ROWCS[0:64, j, g * (TG // 64):(g + 1) * (TG // 64)].unsqueeze(2).broadcast_to([64, TG // 64, 64])
            dve(lambda e, dst=dst, src=src: e.tensor_copy(out=dst[0:64, :].rearrange("p (a b) -> p a b", b=64), in_=src),
                r=[bROPE], w=[bROPE])

    def proj_featmajor_heads(l, tiles, dst, bdst, rope):
        for quad, (tm, tr) in enumerate(tiles):
            banks = {}
            for (which, ti) in (("m", tm), ("r", tr)):
                if ti is None:
                    continue
                W, bW = use_tile(*wtile(win_d, l, 0, 8, ti * 512))
                for pair in range(2):
                    bank, bb = pf()
                    banks[(which, pair)] = (bank, bb)
                    for hh in range(2):
                        cc = pair * 2 + hh
                        for k in range(8):
                            pe(lambda e, W=W, cc=cc, k=k, hh=hh, bank=bank: e.matmul(
                                bank[:, hh * TG:(hh + 1) * TG], lhsT=W[:, k, cc * 128:(cc + 1) * 128], rhs=hT[:, k, 1:TG + 1],
                                start=(k == 0), stop=(k == 7)), r=[bW, bhT], w=[bb])
            for pair in range(2):
                h0 = quad * 4 + pair * 2
                bm, bbm = banks[("m", pair)]
                dv = dst[:, h0:h0 + 2, :]
                if rope:
                    br, bbr = banks[("r", pair)]
                    cosb = COSg[:].unsqueeze(1).broadcast_to([128, 2, TG])
                    sinb = SINg[:].unsqueeze(1).broadcast_to([128, 2, TG])
                    dve(lambda e, bm=bm, cosb=cosb: e.tensor_tensor(out=ropeA[:].rearrange("p (a b) -> p a b", a=2),
                                                                    in0=bm[:].rearrange("p (a b) -> p a b", a=2), in1=cosb, op=ALU.mult),
                        r=[bbm, bROPE], w=[bropeA])
                    dve(lambda e, br=br, sinb=sinb: e.tensor_tensor(out=ropeB[:].rearrange("p (a b) -> p a b", a=2),
                                                                    in0=br[:].rearrange("p (a b) -> p a b", a=2), in1=sinb, op=ALU.mult),
                        r=[bbr, bROPE], w=[bropeB])
                    dve(lambda e, dv=dv: e.tensor_tensor(out=dv, in0=ropeA[:].rearrange("p (a b) -> p a b", a=2),
                                                         in1=ropeB[:].rearrange("p (a b) -> p a b", a=2), op=ALU.add),
                        r=[bropeA, bropeB], w=[bdst])
                else:
                    act(lambda e, dv=dv, bm=bm: e.activation(out=dv, in_=bm[:].rearrange("p (a b) -> p a b", a=2), func=AF.Copy),
                        r=[bbm], w=[bdst])

    def proj_v(l):
        for j in range(4):
            W, bW = use_tile(*wtile(win_d, l, 0, 8, (8 + j) * 512))
            for i in range(G):
                bank, bb = pf()
                for k in range(8):
                    pe(lambda e, W=W, k=k, i=i, bank=bank: e.matmul(bank[:], lhsT=hT[:, k, 1 + i * 128:1 + (i + 1) * 128], rhs=W[:, k, :],
                                                                    start=(k == 0), stop=(k == 7)), r=[bW, bhT], w=[bb])
                act(lambda e, i=i, j=j, bank=bank: e.activation(out=vv[:, i, j * 512:(j + 1) * 512], in_=bank[:], func=AF.Copy),
                    r=[bb], w=[bvv])

    def k_tokmajor(i, dectab):
        s = nxt("kdt")
        pt, bpt = pb()
        for h in range(H):
            pe(lambda e, h=h: e.transpose(out=pt[:, h, :], in_=kT[:, h, i * 128:(i + 1) * 128], identity=identb[:]),
               r=[bkT, bCONST], w=[bpt])
        dve(lambda e, s=s: e.tensor_tensor(out=kdt[s][:], in0=pt[:], in1=dectab[:].unsqueeze(2).broadcast_to([128, H, 128]), op=ALU.mult),
            r=[bpt, bTAB], w=[bkdt[s]])
        return s

    def state_update(i, s, Srun, bSrun, cd_ap):
        for hp in range(4):
            bank, bb = pf()
            for hh in range(2):
                h = hp * 2 + hh
                pe(lambda e, h=h, hh=hh, bank=bank: e.matmul(bank[:, hh * DV:(hh + 1) * DV], lhsT=kdt[s][:, h, :],
                                                             rhs=vv[:, i, h * DV:(h + 1) * DV], start=True, stop=True),
                   r=[bkdt[s], bvv], w=[bb])
            for hh in range(2):
                h = hp * 2 + hh
                mode, sc = cd_ap(h)
                if mode == "decay":
                    dve(lambda e, h=h, hh=hh, bank=bank, sc=sc: e.scalar_tensor_tensor(
                        out=Srun[:, h, :], in0=Srun[:, h, :], scalar=sc, in1=bank[:, hh * DV:(hh + 1) * DV], op0=ALU.mult, op1=ALU.add),
                        r=[bb, bTAB, bSrun], w=[bSrun])
                else:
                    dve(lambda e, h=h, hh=hh, bank=bank, sc=sc: e.scalar_tensor_tensor(
                        out=Srun[:, h, :], in0=bank[:, hh * DV:(hh + 1) * DV], scalar=sc, in1=Srun[:, h, :], op0=ALU.mult, op1=ALU.add),
                        r=[bb, bTAB, bSrun], w=[bSrun])

    sb_cur = [0]

    def pass_b(seq, l, fwd_closed):
        for g in reversed(range(seq.ng)):
            arena_barrier()
            for i in range(G):
                c = g * G + i
                norm_T(seq.Xt[:, c, :], seq.bXt[c], l, 0, seq.bidx, hT, bhT, 1 + i * 128, xs, bxs)
            hg = seq.halo0 + g
            act(lambda e, hg=hg: e.activation(out=HALO[:, :, hg, 0:1], in_=hT[:, :, 1:2], func=AF.Copy), r=[bhT], w=[bHALO])
            act(lambda e, hg=hg: e.activation(out=HALO[:, :, hg, 1:2], in_=hT[:, :, TG:TG + 1], func=AF.Copy), r=[bhT], w=[bHALO])
            if seq.rope:
                rope_tables(g)
                proj_featmajor_heads(l, [(4, 5), (6, 7)], kT, bkT, True)
            else:
                proj_featmajor_heads(l, [(4, None), (6, None)], kT, bkT, False)
            proj_v(l)
            for i in reversed(range(G)):
                c = g * G + i
                s = k_tokmajor(i, KDB)
                cs = sb_cur[0]
                dma("sp", sbd_d[seq.sbd0 + c], Sbb[cs][:], r=[bSbb[cs]], w=[bSBD[seq.sbd0 + c]])
                state_update(i, s, SBr, bSBr, lambda h: ("decay", CDB[:, h:h + 1]))
                ns = 1 - cs
                act(lambda e, ns=ns: e.activation(out=Sbb[ns][:], in_=SBr[:].rearrange("p h v -> p (h v)"), func=AF.Copy),
                    r=[bSBr], w=[bSbb[ns]])
                sb_cur[0] = ns
                if fwd_closed:
                    s2 = k_tokmajor(i, KDF)
                    last = (c == seq.n - 1)
                    state_update(i, s2, SFr, bSFr, (lambda h: ("acc", 1.0)) if last else (lambda h: ("acc", CDF[:, h:h + 1])))

    def main_pass(seq, l, b, final):
        assert seq.n - 1 <= 1 or True
        for g in range(seq.ng):
            arena_barrier()
            for i in range(G):
                c = g * G + i
                norm_T(seq.Xt[:, c, :], seq.bXt[c], l, 0, seq.bidx, hT, bhT, 1 + i * 128, xs, bxs)
            hg = seq.halo0 + g
            if g == 0:
                dve(lambda e: e.memset(hT[:, :, 0:1], 0.0), w=[bhT])
            else:
                act(lambda e, hg=hg: e.activation(out=hT[:, :, 0:1], in_=HALO[:, :, hg - 1, 1:2], func=AF.Copy), r=[bHALO], w=[bhT])
            if g == seq.ng - 1:
                dve(lambda e: e.memset(hT[:, :, TG + 1:TG + 2], 0.0), w=[bhT])
            else:
                act(lambda e, hg=hg: e.activation(out=hT[:, :, TG + 1:TG + 2], in_=HALO[:, :, hg + 1, 0:1], func=AF.Copy), r=[bHALO], w=[bhT])
            if seq.rope:
                rope_tables(g)
                proj_featmajor_heads(l, [(0, 1), (2, 3)], qT, bqT, True)
                proj_featmajor_heads(l, [(4, 5), (6, 7)], kT, bkT, True)
            else:
                proj_featmajor_heads(l, [(0, None), (2, None)], qT, bqT, False)
                proj_featmajor_heads(l, [(4, None), (6, None)], kT, bkT, False)
            proj_v(l)
            for i in range(G):
                c = g * G + i
                dma("sp", Sbb[i][:], sbd_d[seq.sbd0 + c], r=[bSBD[seq.sbd0 + c]], w=[bSbb[i]])
            for i in range(G):
                c = g * G + i
                tok = slice(i * 128, (i + 1) * 128)
                dve(lambda e, tok=tok: e.tensor_tensor(out=qf[:], in0=qT[:, :, tok], in1=QDF[:], op=ALU.mult), r=[bqT, bTAB], w=[bqf])
                dve(lambda e, tok=tok: e.tensor_tensor(out=qb[:], in0=qT[:, :, tok], in1=QDB[:], op=ALU.mult), r=[bqT, bTAB], w=[bqb])
                if seq.kind == "ctx" and i == 1:
                    chk(4.415)
                s = k_tokmajor(i, KDF)
                if seq.kind == "ctx" and i == 1:
                    chk(4.416)
                ps = nxt("PT")
                for quad in range(2):
                    bank, bb = pf()
                    for hh in range(4):
                        h = quad * 4 + hh
                        pe(lambda e, h=h, hh=hh, bank=bank, tok=tok: e.matmul(bank[:, hh * 128:(hh + 1) * 128], lhsT=kT[:, h, tok],
                                                                              rhs=qT[:, h, tok], start=True, stop=True),
                           r=[bkT, bqT], w=[bb])
                    dve(lambda e, quad=quad, bank=bank, ps=ps: e.tensor_tensor(
                        out=PT[ps][:, quad * 4:(quad + 1) * 4, :], in0=bank[:].rearrange("p (a b) -> p a b", a=4),
                        in1=MT[:, quad * 4:(quad + 1) * 4, :], op=ALU.mult), r=[bb, bTAB], w=[bPT[ps]])
                if seq.kind == "ctx" and i == 1:
                    chk(4.417)
                obanks = []
                for hp in range(4):
                    bank, bb = pf()
                    obanks.append((bank, bb))
                    for hh in range(2):
                        h = hp * 2 + hh
                        o_ = bank[:, hh * DV:(hh + 1) * DV]
                        pe(lambda e, h=h, o_=o_, ps=ps, i=i: e.matmul(o_, lhsT=PT[ps][:, h, :], rhs=vv[:, i, h * DV:(h + 1) * DV], start=True, stop=False),
                           r=[bPT[ps], bvv], w=[bb])
                        pe(lambda e, h=h, o_=o_: e.matmul(o_, lhsT=qf[:, h, :], rhs=Sfb[:, h * DV:(h + 1) * DV], start=False, stop=False),
                           r=[bqf, bSfb], w=[bb])
                        pe(lambda e, h=h, o_=o_, i=i: e.matmul(o_, lhsT=qb[:, h, :], rhs=Sbb[i][:, h * DV:(h + 1) * DV], start=False, stop=True),
                           r=[bqb, bSbb[i]], w=[bb])
                if seq.kind == "ctx" and i == 1:
                    chk(4.42)
                s1 = sm[:, 32:40]
                s2 = sm[:, 40:48]
                for hp in range(4):
                    bank, bb = obanks[hp]
                    for hh in range(2):
                        h = hp * 2 + hh
                        act(lambda e, h=h, hh=hh, bank=bank: e.activation(out=onb[:, h * DV:(h + 1) * DV], in_=bank[:, hh * DV:(hh + 1) * DV],
                                                                          func=AF.Identity, accum_out=s1[:, h:h + 1]), r=[bb], w=[bonb, bSM])
                        dve(lambda e, h=h: e.scalar_tensor_tensor(out=ropeA[:, 0:DV], in0=onb[:, h * DV:(h + 1) * DV], scalar=1.0,
                                                                  in1=onb[:, h * DV:(h + 1) * DV], op0=ALU.mult, op1=ALU.mult,
                                                                  accum_out=s2[:, h:h + 1]), r=[bonb], w=[bropeA, bSM])
                accfence("act")
                accfence("dve")
                veps = sm[:, 8:16]
                rstd = sm[:, 16:24]
                nmr = sm[:, 24:32]
                mean = sm[:, 56:64 - 1] if False else mv[:, :, 0]
                act(lambda e: e.activation(out=mv[:, :, 0], in_=s1, func=AF.Identity, scale=1.0 / DV), r=[bSM], w=[bSM])
                dve(lambda e: e.tensor_tensor(out=mv[:, :, 1], in0=mv[:, :, 0], in1=mv[:, :, 0], op=ALU.mult), r=[bSM], w=[bSM])
                dve(lambda e: e.scalar_tensor_tensor(out=veps, in0=s2, scalar=1.0 / DV, in1=mv[:, :, 1], op0=ALU.mult, op1=ALU.subtract), r=[bSM], w=[bSM])
                dve(lambda e: e.tensor_scalar(out=veps, in0=veps, scalar1=EPS, scalar2=None, op0=ALU.add), r=[bSM], w=[bSM])
                rsqrt(rstd, veps, 8, [bSM], bSM)
                dve(lambda e: e.scalar_tensor_tensor(out=nmr, in0=mv[:, :, 0], scalar=-1.0, in1=rstd, op0=ALU.mult, op1=ALU.mult), r=[bSM], w=[bSM])
                for hp in range(4):
                    bank, bb = obanks[hp]
                    for hh in range(2):
                        h = hp * 2 + hh
                        act(lambda e, h=h, hh=hh, bank=bank: e.activation(out=onb[:, h * DV:(h + 1) * DV], in_=bank[:, hh * DV:(hh + 1) * DV],
                                                                          func=AF.Identity, scale=rstd[:, h:h + 1], bias=nmr[:, h:h + 1]),
                            r=[bb, bSM], w=[bonb])
                for half in range(2):
                    pt, bpt = pb()
                    for kk in range(8):
                        vc = half * 8 + kk
                        pe(lambda e, kk=kk, vc=vc, pt=pt: e.transpose(out=pt[:, kk, :], in_=onb[:, vc * 128:(vc + 1) * 128], identity=identb[:]),
                           r=[bonb, bCONST], w=[bpt])
                    act(lambda e, half=half, pt=pt, tok=tok: e.activation(out=onT[:, half * 8:(half + 1) * 8, tok], in_=pt[:], func=AF.Copy),
                        r=[bpt], w=[bonT])
                if seq.kind == "ctx" and i == 0:
                    chk(4.4)
                state_update(i, s, SFr, bSFr, lambda h: ("decay", CDF[:, h:h + 1]))
                act(lambda e: e.activation(out=Sfb[:], in_=SFr[:].rearrange("p h v -> p (h v)"), func=AF.Copy), r=[bSFr], w=[bSfb])
                if seq.kind == "ctx" and i == 0:
                    chk(4.41)
            chk(4.45 if seq.kind == "ctx" else 5.45)
            for j in range(4):
                W, bW = use_tile(*wtile(win_d, l, 0, 8, (12 + j) * 512))
                for pair in range(2):
                    bank, bb = pf()
                    for hh in range(2):
                        cc = pair * 2 + hh
                        for k in range(8):
                            pe(lambda e, W=W, cc=cc, k=k, hh=hh, bank=bank: e.matmul(bank[:, hh * TG:(hh + 1) * TG], lhsT=W[:, k, cc * 128:(cc + 1) * 128],
                                                                                     rhs=hT[:, k, 1:TG + 1], start=(k == 0), stop=(k == 7)),
                               r=[bW, bhT], w=[bb])
                    s = nxt("sgt")
                    act(lambda e, s=s, bank=bank: e.activation(out=sgt[s][:], in_=bank[:], func=AF.Silu), r=[bb], w=[bsgt[s]])
                    vc0 = j * 4 + pair * 2
                    dve(lambda e, s=s, vc0=vc0: e.tensor_tensor(out=onT[:, vc0:vc0 + 2, :], in0=onT[:, vc0:vc0 + 2, :],
                                                                in1=sgt[s][:].rearrange("p (a b) -> p a b", a=2), op=ALU.mult),
                        r=[bsgt[s], bonT], w=[bonT])
            chk(4.5 if seq.kind == "ctx" else 5.5)
            arena_barrier()
            for j in range(4):
                W, bW = use_tile(*wtile(win_d, l, 0, 8, (22 + j) * 512))
                dst, bd = (tgr, btgr) if j % 2 == 0 else (tgc, btgc)
                ch = j // 2
                for i in range(G):
                    bank, bb = pf()
                    for k in range(8):
                        pe(lambda e, W=W, k=k, i=i, bank=bank: e.matmul(bank[:], lhsT=hT[:, k, 1 + i * 128:1 + (i + 1) * 128], rhs=W[:, k, :],
                                                                        start=(k == 0), stop=(k == 7)), r=[bW, bhT], w=[bb])
                    act(lambda e, i=i, ch=ch, bank=bank, dst=dst: e.activation(out=dst[:, i, ch * 512:(ch + 1) * 512], in_=bank[:], func=AF.Tanh, scale=0.5),
                        r=[bb], w=[bd])
            for ch in range(2):
                banks = [pf() for _ in range(G)]
                for kg in range(2):
                    W, bW = use_tile(*wtile(wro_d, l, kg * 8, 8, ch * 512))
                    for i in range(G):
                        bank, bb = banks[i]
                        for k in range(8):
                            pe(lambda e, W=W, k=k, i=i, kg=kg, bank=bank: e.matmul(bank[:], lhsT=onT[:, kg * 8 + k, i * 128:(i + 1) * 128], rhs=W[:, k, :],
                                                                                   start=(kg == 0 and k == 0), stop=(kg == 1 and k == 7)),
                               r=[bW, bonT], w=[bb])
                for i in range(G):
                    bank, bb = banks[i]
                    dve(lambda e, i=i, ch=ch, bank=bank: e.scalar_tensor_tensor(out=macc[:, i, ch * 512:(ch + 1) * 512], in0=tgr[:, i, ch * 512:(ch + 1) * 512],
                                                                                scalar=1.0, in1=bank[:], op0=ALU.add, op1=ALU.mult),
                        r=[bb, btgr], w=[bmacc])
            for t in range(2):
                W, bW = use_tile(*wtile(win_d, l, 0, 8, (16 + t) * 512))
                for cc in range(4):
                    chn = t * 4 + cc
                    bank, bb = pf()
                    for k in range(8):
                        pe(lambda e, W=W, cc=cc, k=k, bank=bank: e.matmul(bank[:, 0:TG + 2], lhsT=W[:, k, cc * 128:(cc + 1) * 128], rhs=hT[:, k, :],
                                                                          start=(k == 0), stop=(k == 7)), r=[bW, bhT], w=[bb])
                    act(lambda e, chn=chn, bank=bank: e.activation(out=uT[:, chn, :], in_=bank[:, 0:TG + 2], func=AF.Copy), r=[bb], w=[buT])
            for t in range(2):
                W, bW = use_tile(*wtile(win_d, l, 0, 8, (18 + t) * 512))
                for cc in range(4):
                    chn = t * 4 + cc
                    bank, bb = pf()
                    for k in range(8):
                        pe(lambda e, W=W, cc=cc, k=k, bank=bank: e.matmul(bank[:, 0:TG + 2], lhsT=W[:, k, cc * 128:(cc + 1) * 128], rhs=hT[:, k, :],
                                                                          start=(k == 0), stop=(k == 7)), r=[bW, bhT], w=[bb])
                    dve(lambda e, chn=chn, bank=bank: e.tensor_tensor(out=uT[:, chn, :], in0=bank[:, 0:TG + 2], in1=uT[:, chn, :], op=ALU.mult),
                        r=[bb, buT], w=[buT])
                    s = nxt("cvt")
                    dve(lambda e, chn=chn, s=s: e.tensor_scalar(out=cvt[s][:], in0=uT[:, chn, 0:TG], scalar1=CW[:, l, 0, chn:chn + 1], scalar2=None,
                                                                op0=ALU.mult), r=[buT, bMOD], w=[bcvt[s]])
                    dve(lambda e, chn=chn, s=s: e.scalar_tensor_tensor(out=cvt[s][:], in0=uT[:, chn, 1:TG + 1], scalar=CW[:, l, 1, chn:chn + 1],
                                                                       in1=cvt[s][:], op0=ALU.mult, op1=ALU.add), r=[buT, bMOD, bcvt[s]], w=[bcvt[s]])
                    dve(lambda e, chn=chn, s=s: e.scalar_tensor_tensor(out=cv[:, chn, :], in0=uT[:, chn, 2:TG + 2], scalar=CW[:, l, 2, chn:chn + 1],
                                                                       in1=cvt[s][:], op0=ALU.mult, op1=ALU.add), r=[buT, bMOD, bcvt[s]], w=[bcv])
            for t in range(2):
                W, bW = use_tile(*wtile(win_d, l, 0, 8, (20 + t) * 512))
                for pair in range(2):
                    bank, bb = pf()
                    for hh in range(2):
                        cc = pair * 2 + hh
                        for k in range(8):
                            pe(lambda e, W=W, cc=cc, k=k, hh=hh, bank=bank: e.matmul(bank[:, hh * TG:(hh + 1) * TG], lhsT=W[:, k, cc * 128:(cc + 1) * 128],
                                                                                     rhs=hT[:, k, 1:TG + 1], start=(k == 0), stop=(k == 7)),
                               r=[bW, bhT], w=[bb])
                    c0 = t * 4 + pair * 2
                    dve(lambda e, c0=c0, bank=bank: e.tensor_tensor(out=ycT[:, c0:c0 + 2, :], in0=bank[:].rearrange("p (a b) -> p a b", a=2),
                                                                    in1=cv[:, c0:c0 + 2, :], op=ALU.mult), r=[bb, bcv], w=[bycT])
            for ch in range(2):
                W, bW = use_tile(*wtile(wco_d, l, 0, 8, ch * 512))
                for i in range(G):
                    bank, bb = pf()
                    for k in range(8):
                        pe(lambda e, W=W, k=k, i=i, bank=bank: e.matmul(bank[:], lhsT=ycT[:, k, i * 128:(i + 1) * 128], rhs=W[:, k, :],
                                                                        start=(k == 0), stop=(k == 7)), r=[bW, bycT], w=[bb])
                    s = nxt("t2")
                    dve(lambda e, i=i, ch=ch, s=s, bank=bank: e.scalar_tensor_tensor(out=t2[s][:], in0=tgc[:, i, ch * 512:(ch + 1) * 512], scalar=1.0,
                                                                                     in1=bank[:], op0=ALU.add, op1=ALU.mult), r=[bb, btgc], w=[bt2[s]])
                    dve(lambda e, i=i, ch=ch, s=s: e.tensor_tensor(out=mm[:, i, ch * 512:(ch + 1) * 512], in0=t2[s][:], in1=macc[:, i, ch * 512:(ch + 1) * 512],
                                                                   op=ALU.add), r=[bt2[s], bmacc], w=[bmm])
            for i in range(G):
                pt, bpt = pb()
                for k in range(8):
                    pe(lambda e, k=k, i=i, pt=pt: e.transpose(out=pt[:, k, :], in_=mm[:, i, k * 128:(k + 1) * 128], identity=identb[:]),
                       r=[bmm, bCONST], w=[bpt])
                act(lambda e, i=i, pt=pt: e.activation(out=mT[:, :, i * 128:(i + 1) * 128], in_=pt[:], func=AF.Copy), r=[bpt], w=[bmT])
            for ch in range(2):
                W, bW = use_tile(*wtile(wo_d, l, 0, 8, ch * 512))
                for i in range(G):
                    c = g * G + i
                    bank, bb = pf()
                    for k in range(8):
                        pe(lambda e, W=W, k=k, i=i, bank=bank: e.matmul(bank[:], lhsT=mT[:, k, i * 128:(i + 1) * 128], rhs=W[:, k, :],
                                                                        start=(k == 0), stop=(k == 7)), r=[bW, bmT], w=[bb])
                    s = nxt("wtC")
                    xsl_ = seq.Xt[:, c, ch * 512:(ch + 1) * 512]
                    dve(lambda e, ch=ch, s=s, bank=bank: e.tensor_tensor(out=wtmpC[s][:], in0=bank[:], in1=G2[:, ch * 512:(ch + 1) * 512], op=ALU.mult),
                        r=[bb, bG2], w=[bwtC[s]])
                    dve(lambda e, s=s, xsl_=xsl_: e.tensor_tensor(out=xsl_, in0=xsl_, in1=wtmpC[s][:], op=ALU.add),
                        r=[bwtC[s], seq.bXt[c]], w=[seq.bXt[c]])
            chk(4.7 if seq.kind == "ctx" else 5.7)
            arena_barrier()
            for i in range(G):
                c = g * G + i
                norm_T(seq.Xt[:, c, :], seq.bXt[c], l, 1, seq.bidx, h2T, bh2T, i * 128, xsF, bxsF)
            for t in range(11):
                W, bW = use_tile(*wtile(wfi_d, l, 0, 8, t * 512))
                pbanks = []
                for pair in range(2):
                    bank, bb = pf()
                    pbanks.append((bank, bb))
                    for hh in range(2):
                        cc = pair * 2 + hh
                        for k in range(8):
                            pe(lambda e, W=W, cc=cc, k=k, hh=hh, bank=bank: e.matmul(bank[:, hh * TG:(hh + 1) * TG], lhsT=W[:, k, cc * 128:(cc + 1) * 128],
                                                                                     rhs=h2T[:, k, :], start=(k == 0), stop=(k == 7)),
                               r=[bW, bh2T], w=[bb])
                s = nxt("sgF")
                (bkg, bbg), (bku, bbu) = pbanks
                act(lambda e, s=s, bkg=bkg: e.activation(out=sgtF[s][:], in_=bkg[:], func=AF.Silu), r=[bbg], w=[bsgF[s]])
                dve(lambda e, s=s, t=t, bku=bku: e.tensor_tensor(out=hidT[:, 2 * t:2 * t + 2, :], in0=bku[:].rearrange("p (a b) -> p a b", a=2),
                                                                 in1=sgtF[s][:].rearrange("p (a b) -> p a b", a=2), op=ALU.mult),
                    r=[bbu, bsgF[s]], w=[bhidT])
            for ch in range(2):
                banks = [pf() for _ in range(G)]
                for kg in range(3):
                    nk = 8 if kg < 2 else NKF - 16
                    W, bW = use_tile(*wtile(wfo_d, l, kg * 8, nk, ch * 512))
                    for i in range(G):
                        bank, bb = banks[i]
                        for k in range(nk):
                            pe(lambda e, W=W, k=k, i=i, kg=kg, nk=nk, bank=bank: e.matmul(bank[:], lhsT=hidT[:, kg * 8 + k, i * 128:(i + 1) * 128], rhs=W[:, k, :],
                                                                                          start=(kg == 0 and k == 0), stop=(kg == 2 and k == nk - 1)),
                               r=[bW, bhidT], w=[bb])
                for i in range(G):
                    c = g * G + i
                    bank, bb = banks[i]
                    s = nxt("wtF")
                    xsl_ = seq.Xt[:, c, ch * 512:(ch + 1) * 512]
                    dve(lambda e, ch=ch, s=s, bank=bank: e.tensor_tensor(out=wtmpF[s][:], in0=bank[:], in1=G5[:, ch * 512:(ch + 1) * 512], op=ALU.mult),
                        r=[bb, bG5], w=[bwtF[s]])
                    dve(lambda e, s=s, xsl_=xsl_: e.tensor_tensor(out=xsl_, in0=xsl_, in1=wtmpF[s][:], op=ALU.add),
                        r=[bwtF[s], seq.bXt[c]], w=[seq.bXt[c]])
            if final:
                for i in range(G):
                    c = g * G + i
                    xt = seq.Xt[:, c, :]
                    ssv, vv_, rs = sm[:, 0:1], sm[:, 1:2], sm[:, 2:3]
                    s = nxt("Yt")
                    act(lambda e, xt=xt, s=s: e.activation(out=Yt[s][:], in_=xt, func=AF.Square, accum_out=ssv), r=[seq.bXt[c]], w=[bYt[s], bSM])
                    accfence("act")
                    act(lambda e: e.activation(out=vv_, in_=ssv, func=AF.Identity, scale=1.0 / D, bias=EPS), r=[bSM], w=[bSM])
                    rsqrt(rs, vv_, 1, [bSM], bSM)
                    act(lambda e, xt=xt, s=s: e.activation(out=Yt[s][:], in_=xt, func=AF.Identity, scale=rs), r=[seq.bXt[c], bSM], w=[bYt[s]])
                    dve(lambda e, s=s: e.tensor_tensor(out=Yt[s][:], in0=Yt[s][:], in1=FNW[:], op=ALU.mult), r=[bYt[s], bFNW], w=[bYt[s]])
                    o = dma("sp", out_d[b, c * 128:(c + 1) * 128, :], Yt[s][:], r=[bYt[s]])
                    if o is not None:
                        finals.append(o)

    def chk(level):
        if stop <= level:
            raise _Stop()

    def program():
        try:
            program_()
        except _Stop:
            pass
        if stop < 99:
            dump("MTa", MTa[:], [bMOD], [128, 2, 48, NBC])
            dump("WM", WM[:], [bMOD], [128, 2, 2, 8 * NBC])
            dump("MT", MT[:], [bTAB], [128, H, 128])
            dump("KD", KDF[:], [bTAB], [128, H])
            dump("KDB", KDB[:], [bTAB], [128, H])
            dump("CDF", CDF[:], [bTAB], [128, H])
            dump("QDF", QDF[:], [bTAB], [128, H, 128], BF16)
            dump("QDB", QDB[:], [bTAB], [128, H, 128], BF16)
            dump("hT", hT[:], [bhT], [128, 8, TG + 2], BF16)
            dump("kT", kT[:], [bkT], [128, H, TG], BF16)
            dump("qT", qT[:], [bqT], [128, H, TG], BF16)
            dump("vv", vv[:], [bvv], [128, G, H * DV], BF16)
            dump("SBr", SBr[:], [bSBr], [128, H, DV])
            dump("SFr", SFr[:], [bSFr], [128, H, DV])
            dump("onT", onT[:], [bonT], [128, 16, TG], BF16)
            dump("XC", XC[:], bXC, [128, NCC, D])
            dump("X", X[:], bX, [128, NCH, D])
            dump("G2", G2[:], [bG2], [128, D])
            dump("mm", mm[:], [bmm], [128, G, D], BF16)
            dump("COSg", COSg[:], [bROPE], [128, TG])
            dump("ycT", ycT[:], [bycT], [128, 8, TG], BF16)
            dump("macc", macc[:], [bmacc], [128, G, D])
            dump("Sbb0", Sbb[0][:], [bSbb[0]], [128, H * DV], BF16)
            dump("Sbb1", Sbb[1][:], [bSbb[1]], [128, H * DV], BF16)
            dump("qb", qb[:], [bqb], [128, H, 128], BF16)
            dump("qf", qf[:], [bqf], [128, H, 128], BF16)
            dump("onb", onb[:], [bonb], [128, H * DV], BF16)
            dump("PT0", PT[0][:], [bPT[0]], [128, H, 128], BF16)
            dump("PT1", PT[1][:], [bPT[1]], [128, H, 128], BF16)
            dump("Sfb", Sfb[:], [bSfb], [128, H * DV], BF16)
            dump("onT0", onT[:, :, 0:128], [bonT], [128, 16, 128], BF16)
            dump("sm", sm[:], [bSM], [128, 64])
            dump("smi", smi[:].bitcast(F32), [bSM], [128, 16])
            dump("xs0", xs[0][:], [bxs[0]], [128, D], BF16)

    def program_():
        setup()
        chk(1)
        for b in range(NB):
            dma("sp", XC[:], ctx_d[b].rearrange("(t p) d -> p t d", p=128), w=bXC)
            for t4 in range(NCH // 4):
                dma("sp", X[:, t4 * 4:(t4 + 1) * 4, :], x_d[b, t4 * 512:(t4 + 1) * 512, :].rearrange("(t p) d -> p t d", p=128),
                    w=bX[t4 * 4:(t4 + 1) * 4])
            lat = mkseq("lat", b)
            cx = mkseq("ctx", b)
            for l in range(2):
                last = (l == 1)
                arena_barrier()
                decay_tables(l)
                chk(2 + 10 * l)
                dve(lambda e: e.memset(SBr[:], 0.0), w=[bSBr])
                dve(lambda e, cs=sb_cur[0]: e.memset(Sbb[cs][:], 0.0), w=[bSbb[sb_cur[0]]])
                dve(lambda e: e.memset(SFr[:], 0.0), w=[bSFr])
                pass_b(cx, l, fwd_closed=last)
                chk(3 + 10 * l)
                pass_b(lat, l, fwd_closed=False)
                chk(4 + 10 * l)
                if not last:
                    arena_barrier()
                    gate_rows(l, NB)
                    act(lambda e: e.activation(out=Sfb[:], in_=SFr[:].rearrange("p h v -> p (h v)"), func=AF.Copy), r=[bSFr], w=[bSfb])
                    main_pass(cx, l, b, final=False)
                    chk(5 + 10 * l)
                else:
                    act(lambda e: e.activation(out=Sfb[:], in_=SFr[:].rearrange("p h v -> p (h v)"), func=AF.Copy), r=[bSFr], w=[bSfb])
                arena_barrier()
                gate_rows(l, b)
                main_pass(lat, l, b, final=last)
                chk(6 + 10 * l)

    P.dry = True
    program()
    P.dry = False
    rr["pf"] = rr["pb"] = 0
    slot.clear()
    sb_cur[0] = 0
    program()
    P.emit(final_ops=finals)
    return nc, dbg_out


def _rot_perm():
    p = np.arange(128)
    return np.where((p % 64) < 32, p + 32, p - 32)


def host_layout(inp, b0, NB):
    f = lambda a: np.ascontiguousarray(a, dtype=np.float32)
    w_in = inp["w_in"]
    perm = _rot_perm()
    cols = []
    for base in (0, 1024):
        for half in range(2):
            hs = range(half * 4, half * 4 + 4)
            cols.append(np.concatenate([base + h * 128 + np.arange(128) for h in hs]))
            cols.append(np.concatenate([base + h * 128 + perm for h in hs]))
    cols.append(np.arange(2048, 4096))
    cols.append(np.arange(4096, 6144))
    cols.append(np.arange(8192, 9216))
    cols.append(np.arange(7168, 8192))
    cols.append(np.arange(6144, 7168))
    gr, gc = 9216, 10240
    cols.append(np.concatenate([gr + np.arange(512), gc + np.arange(512), gr + 512 + np.arange(512), gc + 512 + np.arange(512)]))
    cols = np.concatenate(cols)
    assert cols.shape[0] == 13312
    w_in_ext = f(w_in[:, :, cols])
    fcols = []
    for t in range(11):
        for f0 in (2 * t, 2 * t + 1):
            fcols.append(f0 * 128 + np.arange(128))
        for f0 in (2 * t, 2 * t + 1):
            fcols.append(DFF + f0 * 128 + np.arange(128))
    fcols = np.concatenate(fcols)
    w_ffn_in_ext = f(inp["w_ffn_in"][:, :, fcols])
    c5 = np.concatenate([inp["c"][b0:b0 + NB], inp["c_ctx"][None, :]], axis=0)
    cT = f(c5.T.reshape(8, 128, NB + 1).transpose(1, 0, 2))
    fm = lambda a: f(a.reshape(a.shape[0], -1, 128).transpose(2, 0, 1))
    convwT = f(inp["conv_w"].reshape(2, 3, 8, 128).transpose(3, 0, 1, 2))
    fr = 10000.0 ** (-np.arange(0, 64, 2, dtype=np.float32) / 64.0)
    p = np.arange(128)
    fp = fr[p % 32]
    sign = np.where((p % 64) < 32, -1.0, 1.0)
    rows = np.arange(32, dtype=np.float32)
    cl = np.arange(64, dtype=np.float32)
    rowcs = np.stack([np.cos(fp[:, None] * rows[None, :]), sign[:, None] * np.sin(fp[:, None] * rows[None, :])], axis=1)
    colcs = np.stack([np.cos(fp[:, None] * cl[None, :]), sign[:, None] * np.sin(fp[:, None] * cl[None, :])], axis=1)
    return {
        "x": f(inp["x"][b0:b0 + NB]), "ctx": f(inp["ctx"][b0:b0 + NB]), "cT": cT,
        "ada_w": f(inp["ada_w"]), "ada_bT": fm(inp["ada_b"]), "nmwT": fm(inp["norm_mix_w"]), "nfwT": fm(inp["norm_ffn_w"]),
        "convwT": convwT, "fnw": f(inp["final_norm_w"]), "decay": f(inp["ret_decay_logit"].reshape(32)),
        "rowcs": f(rowcs), "colcs": f(colcs), "ident": np.eye(128, dtype=np.float32),
        "w_in_ext": w_in_ext, "w_ret_out": f(inp["w_ret_out"]), "w_conv_out": f(inp["w_conv_out"]), "w_o": f(inp["w_o"]),
        "w_ffn_in_ext": w_ffn_in_ext, "w_ffn_out": f(inp["w_ffn_out"]),
    }


def kernel(**inputs):
    inputs = {k: np.asarray(v) for k, v in inputs.items()}
    Bt = inputs["x"].shape[0]
    NB = Bt // N_CORES
    nc, _ = build(NB)
    shared = None
    in_maps = []
    for core in range(N_CORES):
        m = host_layout(inputs, core * NB, NB) if shared is None else None
        if shared is None:
            shared = m
        else:
            m = dict(shared)
            b0 = core * NB
            m["x"] = np.ascontiguousarray(inputs["x"][b0:b0 + NB], dtype=np.float32)
            m["ctx"] = np.ascontiguousarray(inputs["ctx"][b0:b0 + NB], dtype=np.float32)
            c5 = np.concatenate([inputs["c"][b0:b0 + NB], inputs["c_ctx"][None, :]], axis=0)
            m["cT"] = np.ascontiguousarray(c5.T.reshape(8, 128, NB + 1).transpose(1, 0, 2), dtype=np.float32)
        in_maps.append(m)
    res = run_bass_kernel_spmd(nc, in_maps, core_ids=list(range(N_CORES)))
    return np.concatenate([r["out"] for r in res.results], axis=0).astype(np.float32)
```
